# Optimizing a Trainium2 kernel written in Bass

```python
import math
import jax, jax.numpy as jnp
from jax import lax
import numpy as np

D_MODEL = 2048
BATCH = 1
SEQ = 8192
DEPTH = 4

HEAD_DIM = 64
BLOCK = 128
A_HEADS = 12
A_LATENT = 128
IDX_HEADS = 16
IDX_DIM = 64
A_TOPK_MAX = 256
B_GROUPS = ((128, 1), (512, 4), (2048, 16))
B_HEADS_PER_GROUP = 4
B_HEADS = B_HEADS_PER_GROUP * len(B_GROUPS)
B_OUT = B_HEADS_PER_GROUP * HEAD_DIM
C_HEADS = 8
C_KV_HEADS = 2
C_WINDOW = 128
T5_BUCKETS = 32
T5_MAX_DIST = 2048
T5_HEADS = A_HEADS + B_HEADS + C_HEADS
FFN_DIM = 5504
N_MOD = 9
EPS = 1e-6
NEG = -1e30

IN_WIDTHS = (
    A_HEADS * HEAD_DIM, A_LATENT, IDX_HEADS * IDX_DIM, IDX_DIM, IDX_HEADS,
    B_HEADS * HEAD_DIM, B_HEADS * HEAD_DIM, B_HEADS * HEAD_DIM,
    C_HEADS * HEAD_DIM, C_KV_HEADS * HEAD_DIM, C_KV_HEADS * HEAD_DIM,
    D_MODEL, D_MODEL, D_MODEL,
)
IN_COLS = sum(IN_WIDTHS)
IN_OFFSETS = tuple(int(v) for v in np.cumsum(IN_WIDTHS)[:-1])

kernel_name = "hybrid_dsa_dilated_swa_macaron_adaln"


def rmsnorm(x, g):
    x32 = x.astype(jnp.float32)
    y = x32 * lax.rsqrt(jnp.mean(x32 * x32, axis=-1, keepdims=True) + EPS)
    return (y * g.astype(jnp.float32)).astype(x.dtype)


def t5_bucket(dist):
    max_exact = T5_BUCKETS // 2
    d_large = jnp.maximum(dist, max_exact).astype(jnp.float32)
    large = max_exact + (jnp.log(d_large / max_exact) / math.log(T5_MAX_DIST / max_exact)
                         * (T5_BUCKETS - max_exact)).astype(jnp.int32)
    large = jnp.minimum(large, T5_BUCKETS - 1)
    return jnp.where(dist < max_exact, dist, large)


def swiglu(x, w_in, w_out):
    a, b = jnp.split(x @ w_in, 2, axis=-1)
    return (jax.nn.silu(a) * b) @ w_out


def dsa_mixer(a_q, a_kv, i_q, i_k, i_w, kv_norm, w_uk, w_uv, idx_k_norm, bias_tab):
    bsz, seq, _ = a_q.shape
    topk = min(A_TOPK_MAX, seq // 4)
    nblk = seq // BLOCK
    latent = rmsnorm(a_kv, kv_norm)
    q = a_q.reshape(bsz, seq, A_HEADS, HEAD_DIM)
    q_abs = jnp.einsum("bshd,rhd->bshr", q, w_uk.reshape(A_LATENT, A_HEADS, HEAD_DIM))
    iq = i_q.reshape(bsz, seq, IDX_HEADS, IDX_DIM)
    ik = rmsnorm(i_k, idx_k_norm)
    iw = i_w.astype(jnp.float32) * IDX_HEADS ** -0.5
    key_pos = jnp.arange(seq)

    def block(bi):
        start = bi * BLOCK
        t = start + jnp.arange(BLOCK)
        iq_b = lax.dynamic_slice_in_dim(iq, start, BLOCK, axis=1)
        iw_b = lax.dynamic_slice_in_dim(iw, start, BLOCK, axis=1)
        qa_b = lax.dynamic_slice_in_dim(q_abs, start, BLOCK, axis=1)
        s_h = jnp.einsum("bqhd,bsd->bqhs", iq_b, ik).astype(jnp.float32) * IDX_DIM ** -0.5
        score = jnp.einsum("bqh,bqhs->bqs", iw_b, jax.nn.relu(s_h))
        causal = key_pos[None, :] <= t[:, None]
        score = jnp.where(causal[None], score, NEG)
        _, idx = lax.top_k(score, topk)
        valid = idx <= t[None, :, None]
        lat_g = jax.vmap(lambda lb, ib: lb[ib])(latent, idx)
        logits = jnp.einsum("bqhr,bqkr->bhqk", qa_b, lat_g).astype(jnp.float32) * HEAD_DIM ** -0.5
        bias = bias_tab[t5_bucket(jnp.maximum(t[None, :, None] - idx, 0))]
        logits = logits + jnp.transpose(bias, (0, 3, 1, 2)).astype(jnp.float32)
        logits = jnp.where(valid[:, None], logits, NEG)
        p = jax.nn.softmax(logits, axis=-1).astype(lat_g.dtype)
        return jnp.einsum("bhqk,bqkr->bqhr", p, lat_g)

    o_lat = lax.map(block, jnp.arange(nblk))
    o_lat = jnp.moveaxis(o_lat, 0, 1).reshape(bsz, seq, A_HEADS, A_LATENT)
    out = jnp.einsum("bshr,rhd->bshd", o_lat, w_uv.reshape(A_LATENT, A_HEADS, HEAD_DIM))
    return out.reshape(bsz, seq, A_HEADS * HEAD_DIM)


def dilated_mixer(b_q, b_k, b_v, bias_tab):
    bsz, seq, _ = b_q.shape
    n_g = len(B_GROUPS)
    nblk = seq // BLOCK
    shp = (bsz, seq, n_g, B_HEADS_PER_GROUP, HEAD_DIM)
    q, k, v = b_q.reshape(shp), b_k.reshape(shp), b_v.reshape(shp)
    qs = [q[:, :, g] for g in range(n_g)]
    kgs = [k[:, :, g] for g in range(n_g)]
    vgs = [v[:, :, g] for g in range(n_g)]

    def block(bi):
        start = bi * BLOCK
        t = start + jnp.arange(BLOCK)
        outs, lses = [], []
        for g, (win, dil) in enumerate(B_GROUPS):
            j = jnp.arange(win // dil + 1)
            pos = t[:, None] - dil * j[None, :]
            valid = pos >= 0
            pos = jnp.maximum(pos, 0)
            kg = kgs[g][:, pos]
            vg = vgs[g][:, pos]
            qb = lax.dynamic_slice_in_dim(qs[g], start, BLOCK, axis=1)
            logits = jnp.einsum("bqhd,bqjhd->bhqj", qb, kg).astype(jnp.float32) * HEAD_DIM ** -0.5
            bias = bias_tab[t5_bucket(dil * j)][:, g * B_HEADS_PER_GROUP:(g + 1) * B_HEADS_PER_GROUP]
            logits = logits + bias.T.astype(jnp.float32)[None, :, None, :]
            logits = jnp.where(valid[None, None], logits, NEG)
            lse = jax.nn.logsumexp(logits, axis=-1)
            p = jnp.exp(logits - lse[..., None]).astype(vg.dtype)
            outs.append(jnp.einsum("bhqj,bqjhd->bqhd", p, vg))
            lses.append(lse)
        alpha = jax.nn.softmax(jnp.stack(lses, 0), axis=0)
        alpha = jnp.transpose(alpha, (0, 1, 3, 2))[..., None]
        o = jnp.sum(alpha * jnp.stack(outs, 0).astype(jnp.float32), axis=0)
        return o.astype(b_q.dtype)

    o = lax.map(block, jnp.arange(nblk))
    return jnp.moveaxis(o, 0, 1).reshape(bsz, seq, B_OUT)


def swa_sink_mixer(c_q, c_k, c_v, sinks, bias_tab):
    bsz, seq, _ = c_q.shape
    nb = seq // BLOCK
    grp = C_HEADS // C_KV_HEADS
    q = c_q.reshape(bsz, nb, BLOCK, C_KV_HEADS, grp, HEAD_DIM)
    k = c_k.reshape(bsz, nb, BLOCK, C_KV_HEADS, HEAD_DIM)
    v = c_v.reshape(bsz, nb, BLOCK, C_KV_HEADS, HEAD_DIM)

    def with_prev(a):
        prev = jnp.pad(a, ((0, 0), (1, 0), (0, 0), (0, 0), (0, 0)))[:, :-1]
        return jnp.concatenate([prev, a], axis=2)

    kk, vv = with_prev(k), with_prev(v)
    qi = jnp.arange(BLOCK)
    kj = jnp.arange(2 * BLOCK)
    dist = qi[:, None] + BLOCK - kj[None, :]
    in_win = (dist >= 0) & (dist < C_WINDOW)
    key_ok = (jnp.arange(nb)[:, None] * BLOCK + kj[None, :] - BLOCK) >= 0
    mask = in_win[None] & key_ok[:, None, :]
    bias = bias_tab[t5_bucket(jnp.maximum(dist, 0))]
    bias = jnp.transpose(bias, (2, 0, 1)).reshape(C_KV_HEADS, grp, BLOCK, 2 * BLOCK)
    logits = jnp.einsum("bnqkgd,bnjkd->bnkgqj", q, kk).astype(jnp.float32) * HEAD_DIM ** -0.5
    logits = logits + bias.astype(jnp.float32)
    logits = jnp.where(mask[None, :, None, None], logits, NEG)
    sink = jnp.broadcast_to(sinks.astype(jnp.float32).reshape(C_KV_HEADS, grp, 1, 1),
                            logits.shape[:-1] + (1,))
    p = jax.nn.softmax(jnp.concatenate([logits, sink], axis=-1), axis=-1)[..., :-1]
    o = jnp.einsum("bnkgqj,bnjkd->bnqkgd", p.astype(vv.dtype), vv)
    return o.reshape(bsz, seq, C_HEADS * HEAD_DIM)


def setup_inputs(seed: int = 0) -> dict:
    key = jax.random.key(seed)
    ks = jax.random.split(key, 24)
    f32 = jnp.float32
    L, D = DEPTH, D_MODEL

    def nrm(k, shape, scale):
        return jax.random.normal(k, shape, f32) * scale

    def gain(k, shape):
        return 1.0 + 0.02 * jax.random.normal(k, shape, f32)

    return {
        "x": nrm(ks[0], (BATCH, SEQ, D), 1.0),
        "c": nrm(ks[1], (BATCH, D), 1.0),
        "t5_table": nrm(ks[2], (T5_BUCKETS, T5_HEADS), 0.5),
        "w_ada": nrm(ks[3], (L, D, N_MOD * D), 0.5 * D ** -0.5),
        "b_ada": nrm(ks[4], (L, N_MOD * D), 0.01),
        "norm_ffn1": gain(ks[5], (L, D)),
        "ffn1_w_in": nrm(ks[6], (L, D, 2 * FFN_DIM), D ** -0.5),
        "ffn1_w_out": nrm(ks[7], (L, FFN_DIM, D), FFN_DIM ** -0.5),
        "norm_mix": gain(ks[8], (L, D)),
        "w_in": nrm(ks[9], (L, D, IN_COLS), D ** -0.5),
        "kv_norm": gain(ks[10], (L, A_LATENT)),
        "w_uk": nrm(ks[11], (L, A_LATENT, A_HEADS * HEAD_DIM), A_LATENT ** -0.5),
        "w_uv": nrm(ks[12], (L, A_LATENT, A_HEADS * HEAD_DIM), A_LATENT ** -0.5),
        "idx_k_norm": gain(ks[13], (L, IDX_DIM)),
        "sinks": nrm(ks[14], (L, C_HEADS), 0.5),
        "w_br_a": nrm(ks[15], (L, A_HEADS * HEAD_DIM, D), (A_HEADS * HEAD_DIM) ** -0.5),
        "w_br_b": nrm(ks[16], (L, B_OUT, D), B_OUT ** -0.5),
        "w_br_c": nrm(ks[17], (L, C_HEADS * HEAD_DIM, D), (C_HEADS * HEAD_DIM) ** -0.5),
        "w_out": nrm(ks[18], (L, D, D), D ** -0.5),
        "norm_ffn2": gain(ks[19], (L, D)),
        "ffn2_w_in": nrm(ks[20], (L, D, 2 * FFN_DIM), D ** -0.5),
        "ffn2_w_out": nrm(ks[21], (L, FFN_DIM, D), FFN_DIM ** -0.5),
        "final_norm": gain(ks[22], (D,)),
    }


def reference(x, c, t5_table, w_ada, b_ada, norm_ffn1, ffn1_w_in, ffn1_w_out, norm_mix, w_in,
              kv_norm, w_uk, w_uv, idx_k_norm, sinks, w_br_a, w_br_b, w_br_c, w_out,
              norm_ffn2, ffn2_w_in, ffn2_w_out, final_norm):
    bias_a = t5_table[:, :A_HEADS]
    bias_b = t5_table[:, A_HEADS:A_HEADS + B_HEADS]
    bias_c = t5_table[:, A_HEADS + B_HEADS:]
    cond = jax.nn.silu(c)
    h = x
    for l in range(DEPTH):
        mod = (cond @ w_ada[l] + b_ada[l])[:, None, :]
        sh1, sc1, g1, sh2, sc2, g2, sh3, sc3, g3 = jnp.split(mod, N_MOD, axis=-1)
        u = rmsnorm(h, norm_ffn1[l]) * (1.0 + sc1) + sh1
        h = h + 0.5 * g1 * swiglu(u, ffn1_w_in[l], ffn1_w_out[l])
        u = rmsnorm(h, norm_mix[l]) * (1.0 + sc2) + sh2
        (a_q, a_kv, i_q, i_k, i_w, b_q, b_k, b_v, c_q, c_k, c_v,
         g_a, g_b, g_c) = jnp.split(u @ w_in[l], IN_OFFSETS, axis=-1)
        y_a = dsa_mixer(a_q, a_kv, i_q, i_k, i_w, kv_norm[l], w_uk[l], w_uv[l],
                        idx_k_norm[l], bias_a) @ w_br_a[l]
        y_b = dilated_mixer(b_q, b_k, b_v, bias_b) @ w_br_b[l]
        y_c = swa_sink_mixer(c_q, c_k, c_v, sinks[l], bias_c) @ w_br_c[l]
        merged = jax.nn.sigmoid(g_a) * y_a + jax.nn.sigmoid(g_b) * y_b + jax.nn.sigmoid(g_c) * y_c
        h = h + g2 * (merged @ w_out[l])
        u = rmsnorm(h, norm_ffn2[l]) * (1.0 + sc3) + sh3
        h = h + 0.5 * g3 * swiglu(u, ffn2_w_in[l], ffn2_w_out[l])
    return rmsnorm(h, final_norm)
```

```python
import contextlib
import sys
import time
import numpy as np
import ml_dtypes
import concourse.bass as bass
import concourse.mybir as mybir
from concourse.bass_utils import run_bass_kernel_spmd

F32 = mybir.dt.float32
BF16 = mybir.dt.bfloat16
AF = mybir.ActivationFunctionType
ALU = mybir.AluOpType
AX = mybir.AxisListType

D = 2048
KC = 16
FFN = 5504
FC = 43
NCORES = 8
EPS = 1e-6
A_HEADS = 12
IDX_HEADS = 16
TOPK = 256
BIG = 32768.0
IN_W = (768, 128, 1024, 64, 16, 768, 768, 768, 512, 128, 128, 2048, 2048, 2048)
IN_OFF = [0]
for _w in IN_W:
    IN_OFF.append(IN_OFF[-1] + _w)
(O_AQ, O_AKV, O_IQ, O_IK, O_IW, O_BQ, O_BK, O_BV, O_CQ, O_CK, O_CV, O_GA, O_GB, O_GC, IN_COLS) = IN_OFF


class TU:
    __slots__ = ("w", "r")

    def __init__(self):
        self.w = None
        self.r = {}


class View:
    __slots__ = ("ap", "tu")

    def __init__(self, ap, tu=None):
        self.ap = ap
        self.tu = tu if tu is not None else TU()

    def __getitem__(self, idx):
        return View(self.ap[idx], self.tu)


class Ctx:
    def __init__(self, nc, es):
        self.nc = nc
        self.es = es
        self.es0 = es
        self.E = {"pe": nc.tensor, "act": nc.scalar, "dve": nc.vector, "pool": nc.gpsimd, "sp": nc.sync}
        self.sem = {}
        self.cnt = {}
        for e in ("pe", "act", "dve", "pool"):
            self.sem[e] = es.enter_context(nc.semaphore("s_" + e))
            self.cnt[e] = 0
        self.seen = {e: {} for e in self.E}
        self.n_inst = 0
        self.misc = []

    def chan(self, name):
        if name not in self.sem:
            self.sem[name] = self.es0.enter_context(self.nc.semaphore("d_" + name))
            self.cnt[name] = 0
        return name

    def _deps(self, outs, ins):
        d = {}
        for x in ins:
            w = x.tu.w
            if w is not None and d.get(w[0], 0) < w[1]:
                d[w[0]] = w[1]
        for x in outs:
            w = x.tu.w
            if w is not None and d.get(w[0], 0) < w[1]:
                d[w[0]] = w[1]
            for k, v in x.tu.r.items():
                if d.get(k, 0) < v:
                    d[k] = v
        return d

    def _wait(self, eng, deps):
        seen = self.seen[eng]
        for k, v in deps.items():
            if k == "pe" and eng == "pe":
                continue
            if seen.get(k, 0) >= v:
                continue
            self.E[eng].wait_ge(self.sem[k], v)
            seen[k] = v

    def op(self, eng, fn, outs, ins):
        self._wait(eng, self._deps(outs, ins))
        inst = fn(self.E[eng])
        self.cnt[eng] += 1
        c = self.cnt[eng]
        inst.then_inc(self.sem[eng], 1)
        self.n_inst += 1
        for x in ins:
            if x.tu.r.get(eng, 0) < c:
                x.tu.r[eng] = c
        for x in outs:
            x.tu.w = (eng, c)
            x.tu.r = {}

    def dma(self, q, pairs, outs, ins, chan, **kw):
        self.chan(chan)
        self._wait(q, self._deps(outs, ins))
        for (o, i) in pairs:
            self.E[q].dma_start(out=o, in_=i, **kw).then_inc(self.sem[chan], 16)
            self.cnt[chan] += 16
            self.n_inst += 1
        c = self.cnt[chan]
        for x in ins:
            if x.tu.r.get(chan, 0) < c:
                x.tu.r[chan] = c
        for x in outs:
            x.tu.w = (chan, c)
            x.tu.r = {}
            if chan == "misc":
                self.misc.append(x.tu)

    def misc_done(self):
        c = self.cnt.get("misc", 0)
        for tu in self.misc:
            if tu.w is not None and tu.w[0] == "misc":
                tu.w = ("misc", c)
        self.misc = []

    def barrier(self):
        for e in self.E:
            d = {k: v for k, v in self.cnt.items() if v > 0}
            self._wait(e, d)

    def finish(self):
        d = {k: v for k, v in self.cnt.items() if v > 0}
        self._wait("sp", d)

    def mm(self, out, lhsT, rhs, start, stop):
        self.op("pe", lambda e: e.matmul(out.ap, lhsT=lhsT.ap, rhs=rhs.ap, start=start, stop=stop), [out], [lhsT, rhs])

    def act(self, out, in_, func, bias=None, scale=None, accum=None, extra_in=()):
        kw = {}
        ins = [in_] + list(extra_in)
        outs = [out]
        if bias is not None:
            if isinstance(bias, View):
                kw["bias"] = bias.ap
                ins.append(bias)
            else:
                kw["bias"] = bias
        if scale is not None:
            if isinstance(scale, View):
                kw["scale"] = scale.ap
                ins.append(scale)
            else:
                kw["scale"] = scale
        if accum is not None:
            kw["accum_out"] = accum.ap
            outs.append(accum)
        self.op("act", lambda e: e.activation(out=out.ap, in_=in_.ap, func=func, **kw), outs, ins)

    def ts(self, eng, out, in0, s1, s2, op0, op1=None, accum=None):
        ins = [in0]
        outs = [out]
        a1 = s1
        a2 = s2
        if isinstance(s1, View):
            a1 = s1.ap
            ins.append(s1)
        if isinstance(s2, View):
            a2 = s2.ap
            ins.append(s2)
        kw = {}
        if op1 is not None:
            kw["op1"] = op1
        if accum is not None:
            kw["accum_out"] = accum.ap
            outs.append(accum)
        self.op(eng, lambda e: e.tensor_scalar(out=out.ap, in0=in0.ap, scalar1=a1, scalar2=a2, op0=op0, **kw), outs, ins)

    def tt(self, eng, out, in0, in1, op):
        self.op(eng, lambda e: e.tensor_tensor(out=out.ap, in0=in0.ap, in1=in1.ap, op=op), [out], [in0, in1])

    def stt(self, eng, out, in0, scalar, in1, op0, op1):
        ins = [in0, in1]
        a = scalar
        if isinstance(scalar, View):
            a = scalar.ap
            ins.append(scalar)
        self.op(eng, lambda e: e.scalar_tensor_tensor(out=out.ap, in0=in0.ap, scalar=a, in1=in1.ap, op0=op0, op1=op1), [out], ins)

    def copy(self, eng, out, in_):
        if eng == "act":
            self.op(eng, lambda e: e.copy(out=out.ap, in_=in_.ap), [out], [in_])
        else:
            self.op(eng, lambda e: e.tensor_copy(out=out.ap, in_=in_.ap), [out], [in_])

    def memset(self, eng, out, val):
        self.op(eng, lambda e: e.memset(out.ap, val), [out], [])

    def sb(self, name, shape, dtype):
        return self.es.enter_context(self.nc.sbuf_tensor("sb_" + name, shape, dtype))

    def ps(self, name, shape, dtype=F32):
        return self.es.enter_context(self.nc.psum_tensor("ps_" + name, shape, dtype))


class Res:
    pass


def setup_common(K, TOK):
    R = setup_persist(K, TOK)
    alloc_ffn(K, R)
    return R


@contextlib.contextmanager
def scope(K):
    old = K.es
    with contextlib.ExitStack() as es2:
        K.es = es2
        yield
        K.barrier()
    K.es = old


def setup_persist(K, TOK):
    R = Res()
    R.TOK = TOK
    R.TN = min(512, TOK)
    R.NT = TOK // R.TN
    TN, NT = R.TN, R.NT
    R.psf = [View(K.ps("psf%d" % i, [128, 512], F32)[:]) for i in range(7)]
    R.psb = View(K.ps("psb", [128, 1024], BF16)[:])
    R.ps_i = 0
    R.onesD = View(K.sb("onesD", [128, 128], F32)[:])
    K.memset("pool", R.onesD, 1.0 / D)
    R.epsc = View(K.sb("epsc", [128, 1], F32)[:])
    K.memset("pool", R.epsc, EPS)
    R.ident = View(K.sb("ident", [128, 128], BF16)[:])
    R.modT = View(K.sb("modT", [128, 5 * KC], F32)[:])
    R.bada = View(K.sb("bada", [128, 5 * KC], F32)[:])
    R.cond = View(K.sb("cond", [128, KC], F32)[:])
    R.craw = View(K.sb("craw", [128, KC], F32)[:])
    R.nrm = [View(K.sb("nrm%d" % i, [128, KC], F32)[:]) for i in range(2)]
    R.svec = [View(K.sb("svec%d" % i, [128, KC], F32)[:]) for i in range(2)]
    R.hgv = [View(K.sb("hgv%d" % i, [128, KC], F32)[:]) for i in range(2)]
    return R


def alloc_ffn(K, R):
    TOK, TN, NT = R.TOK, R.TN, R.NT
    R.hT_t = K.sb("hT", [128, KC, TOK], F32)
    R.hT = [[View(R.hT_t[:, c, t * TN:(t + 1) * TN]) for t in range(NT)] for c in range(KC)]
    R.uT_t = K.sb("uT", [128, KC, TOK], BF16)
    R.uT = [[View(R.uT_t[:, c, t * TN:(t + 1) * TN]) for t in range(NT)] for c in range(KC)]
    R.NWB = 3
    R.wb = [View(K.sb("wb%d" % i, [128, 8192], BF16)[:]) for i in range(R.NWB)]
    R.wb_i = 0
    R.sq = [View(K.sb("sq%d" % i, [128, TN], F32)[:]) for i in range(2)]
    R.rstd = View(K.sb("rstd", [128, TN], F32)[:])
    R.GP = 15
    R.gT_t = K.sb("gT", [128, R.GP, TOK], BF16)
    R.gT = [[View(R.gT_t[:, f, t * TN:(t + 1) * TN]) for t in range(NT)] for f in range(R.GP)]
    R.stg = [View(K.sb("stg%d" % i, [128, 512], BF16)[:]) for i in range(4)]
    R.stg_i = 0


def next_wb(R):
    i = R.wb_i
    R.wb_i = (i + 1) % R.NWB
    return R.wb[i], "wb%d" % i


def next_ps(R, n=6):
    i = R.ps_i
    R.ps_i = (i + 1) % n
    return R.psf[i]


def next_stg(R):
    i = R.stg_i
    R.stg_i = (i + 1) % len(R.stg)
    return R.stg[i], "stg%d" % i


def emit_norm(K, R, svec, bvec):
    TN, NT = R.TN, R.NT
    pss = R.psf[6]
    for t in range(NT):
        for c in range(KC):
            sq = R.sq[c % 2]
            K.act(sq, R.hT[c][t], AF.Square)
            K.mm(pss[:, 0:TN], R.onesD, sq, c == 0, c == KC - 1)
        K.act(R.rstd, pss[:, 0:TN], AF.Sqrt, bias=R.epsc[:, 0:1])
        K.op("dve", lambda e: e.reciprocal(out=R.rstd.ap, in_=R.rstd.ap), [R.rstd], [R.rstd])
        for c in range(KC):
            sq = R.sq[c % 2]
            K.stt("dve", sq, R.hT[c][t], svec[:, c:c + 1], R.rstd, ALU.mult, ALU.mult)
            K.act(R.uT[c][t], sq, AF.Identity, bias=bvec[:, c:c + 1])


def emit_ffn(K, R, w_in_d, w_out_d, hg):
    TN, NT = R.TN, R.NT
    win = w_in_d.ap.rearrange("(k p) c -> p k c", p=128)
    wout = w_out_d.ap
    parts = []
    f0 = 0
    while f0 < FC:
        n = min(R.GP, FC - f0)
        if FC - f0 > R.GP and FC - f0 < 2 * R.GP:
            n = (FC - f0 + 1) // 2
        parts.append((f0, n))
        f0 += n
    for (p0, pn) in parts:
        fi = 0
        while fi < pn:
            nf = min(2, pn - fi)
            wbuf, ch = next_wb(R)
            wt = View(wbuf.ap.rearrange("p (k c) -> p k c", k=KC), wbuf.tu)
            ca = (p0 + fi) * 128
            K.dma("pool", [(wt.ap[:, :, 0:nf * 128], win[:, :, ca:ca + nf * 128]),
                           (wt.ap[:, :, 256:256 + nf * 128], win[:, :, FFN + ca:FFN + ca + nf * 128])],
                  [wbuf], [w_in_d], ch)
            for j in range(nf):
                for t in range(NT):
                    pa = next_ps(R)
                    pb = next_ps(R)
                    for k in range(KC):
                        K.mm(pa[:, 0:TN], wt[:, k, j * 128:(j + 1) * 128], R.uT[k][t], k == 0, k == KC - 1)
                    for k in range(KC):
                        K.mm(pb[:, 0:TN], wt[:, k, 256 + j * 128:256 + (j + 1) * 128], R.uT[k][t], k == 0, k == KC - 1)
                    sa = R.sq[(fi + j + t) % 2]
                    K.act(sa, pa[:, 0:TN], AF.Silu)
                    K.tt("dve", R.gT[fi + j][t], sa, pb[:, 0:TN], ALU.mult)
            fi += nf
        for dp in range(KC // 2):
            wbuf, ch = next_wb(R)
            wt = View(wbuf.ap[:, 0:pn * 256].rearrange("p (f c) -> p f c", f=pn), wbuf.tu)
            src = wout[p0 * 128:(p0 + pn) * 128, dp * 256:(dp + 1) * 256].rearrange("(f p) c -> p f c", p=128)
            K.dma("pool", [(wt.ap, src)], [wbuf], [w_out_d], ch)
            for dd in range(2):
                d = dp * 2 + dd
                for t in range(NT):
                    py = next_ps(R)
                    for f in range(pn):
                        K.mm(py[:, 0:TN], wt[:, f, dd * 128:(dd + 1) * 128], R.gT[f][t], f == 0, f == pn - 1)
                    K.stt("dve", R.hT[d][t], py[:, 0:TN], hg[:, d:d + 1], R.hT[d][t], ALU.mult, ALU.add)


def load_vec(K, dst, src_d):
    K.dma("sp", [(dst.ap, src_d.ap)], [dst], [src_d], "misc")


def dram_in(nc, name, shape, dtype=F32):
    return View(nc.dram_tensor(name, list(shape), dtype, kind="ExternalInput").ap())


def dram_out(nc, name, shape, dtype=F32):
    return View(nc.dram_tensor(name, list(shape), dtype, kind="ExternalOutput").ap())


def build_A(TOK):
    nc = bass.Bass("TRN2", target_bir_lowering=False)
    es = contextlib.ExitStack()
    with es:
        K = Ctx(nc, es)
        TN = min(512, TOK)
        NT = TOK // TN
        NB = TOK // 128
        hin = dram_in(nc, "hT_in", [D, TOK])
        c_d = dram_in(nc, "c", [128, KC])
        wada = dram_in(nc, "w_ada", [D, 5 * D])
        bada = dram_in(nc, "b_ada", [128, 5 * KC])
        n1_d = dram_in(nc, "norm_ffn1", [128, KC])
        n2_d = dram_in(nc, "norm_mix", [128, KC])
        f_in = dram_in(nc, "ffn_w_in", [D, 2 * FFN])
        f_out = dram_in(nc, "ffn_w_out", [FFN, D])
        w_in = dram_in(nc, "w_in", [D, IN_COLS])
        kvn_d = dram_in(nc, "kv_norm", [128])
        ikn_d = dram_in(nc, "idx_k_norm", [64])
        ident_d = dram_in(nc, "ident", [128, 128])
        hout = dram_out(nc, "hT_out", [D, TOK])
        o_latT = dram_out(nc, "latT", [128, TOK], BF16)
        o_lat = dram_out(nc, "lat", [TOK, 128], BF16)
        o_ikT = dram_out(nc, "ikT", [64, TOK], BF16)
        o_bkT = dram_out(nc, "bkT", [768, TOK], BF16)
        o_bv = dram_out(nc, "bv", [TOK, 768], BF16)
        o_ckT = dram_out(nc, "ckT", [128, TOK], BF16)
        o_cv = dram_out(nc, "cv", [TOK, 128], BF16)
        o_aqT = dram_out(nc, "aqT", [768, TOK], BF16)
        o_iqT = dram_out(nc, "iqT", [1024, TOK], BF16)
        o_iw = dram_out(nc, "iw", [TOK, 16], F32)
        o_bqT = dram_out(nc, "bqT", [768, TOK], BF16)
        o_cqT = dram_out(nc, "cqT", [512, TOK], BF16)
        o_sgT = dram_out(nc, "sgT", [3 * D, TOK], BF16)

        R = setup_common(K, TOK)
        kvn_bc = View(K.sb("kvn_bc", [128, 128], F32)[:])
        ikn_bc = View(K.sb("ikn_bc", [128, 64], F32)[:])
        tstat = View(K.sb("tstat", [128, 8], F32)[:])
        tjunk = View(K.sb("tjunk", [128, 128], F32)[:])
        tmA = [View(K.sb("tmA%d" % i, [128, 208], BF16)[:]) for i in range(2)]
        tmW = [View(K.sb("tmW%d" % i, [128, 16], F32)[:]) for i in range(2)]

        K.dma("sp", [(R.craw.ap, c_d.ap)], [R.craw], [c_d], "misc")
        K.dma("sp", [(R.bada.ap[:, 0:5 * KC], bada.ap)], [R.bada], [bada], "misc")
        load_vec(K, R.nrm[0], n1_d)
        load_vec(K, R.nrm[1], n2_d)
        K.dma("pool", [(R.ident.ap, ident_d.ap)], [R.ident], [ident_d], "misc")
        K.dma("sp", [(kvn_bc.ap, kvn_d.ap.partition_broadcast(128))], [kvn_bc], [kvn_d], "misc")
        K.dma("sp", [(ikn_bc.ap, ikn_d.ap.partition_broadcast(128))], [ikn_bc], [ikn_d], "misc")
        K.misc_done()
        hsrc = hin.ap.rearrange("(k p) t -> p k t", p=128)
        allh = [R.hT[c][t] for c in range(KC) for t in range(NT)]
        K.dma("sp", [(R.hT_t[:, c, :], hsrc[:, c, :]) for c in range(KC)], allh, [hin], "hload")

        K.act(R.cond, R.craw, AF.Silu)
        emit_mod_body(K, R, wada, 5)
        M = R.modT
        K.stt("dve", R.svec[0], M[:, 16:32], 1.0, R.nrm[0], ALU.add, ALU.mult)
        K.stt("dve", R.svec[1], M[:, 64:80], 1.0, R.nrm[1], ALU.add, ALU.mult)
        K.ts("dve", R.hgv[0], M[:, 32:48], 0.5, None, ALU.mult)

        emit_norm(K, R, R.svec[0], M[:, 0:16])
        emit_ffn(K, R, f_in, f_out, R.hgv[0])
        hdst = hout.ap.rearrange("(k p) t -> p k t", p=128)
        K.dma("sp", [(hdst[:, c, :], R.hT_t[:, c, :]) for c in range(KC)], [hout], allh, "hstore")

        emit_norm(K, R, R.svec[1], M[:, 48:64])

        wsrc = w_in.ap.rearrange("(k p) c -> p k c", p=128)
        fm = [(O_AQ, 6, "copy", o_aqT), (O_IQ, 8, "copy", o_iqT), (O_BQ, 6, "scale", o_bqT), (O_BK, 6, "copy", o_bkT),
              (O_CQ, 4, "scale", o_cqT), (O_CK, 1, "copy", o_ckT), (O_GA, 48, "sig", o_sgT)]
        ev = 0
        for (c0, nch, kind, dst) in fm:
            ch0 = 0
            while ch0 < nch:
                n = min(4, nch - ch0)
                wbuf, chn = next_wb(R)
                wt = View(wbuf.ap.rearrange("p (k c) -> p k c", k=KC), wbuf.tu)
                K.dma("pool", [(wt.ap[:, :, 0:n * 128], wsrc[:, :, c0 + ch0 * 128:c0 + (ch0 + n) * 128])], [wbuf], [w_in], chn)
                for j in range(n):
                    for t in range(NT):
                        ps = next_ps(R)
                        for k in range(KC):
                            K.mm(ps[:, 0:TN], wt[:, k, j * 128:(j + 1) * 128], R.uT[k][t], k == 0, k == KC - 1)
                        st, sch = next_stg(R)
                        if kind == "sig":
                            K.act(st[:, 0:TN], ps[:, 0:TN], AF.Sigmoid)
                        elif kind == "scale":
                            K.ts("dve", st[:, 0:TN], ps[:, 0:TN], 0.125, None, ALU.mult)
                        else:
                            if ev % 2 == 0:
                                K.copy("dve", st[:, 0:TN], ps[:, 0:TN])
                            else:
                                K.copy("act", st[:, 0:TN], ps[:, 0:TN])
                            ev += 1
                        r0 = (ch0 + j) * 128
                        K.dma("sp", [(dst.ap[r0:r0 + 128, t * TN:(t + 1) * TN], st.ap[:, 0:TN])], [dst], [st], sch)
                ch0 += n

        def tok_lhsT(k, tb):
            t = (tb * 128) // TN
            off = tb * 128 - t * TN
            return R.uT[k][t][:, off:off + 128]

        wbuf, chn = next_wb(R)
        wt = View(wbuf.ap.rearrange("p (k c) -> p k c", k=KC), wbuf.tu)
        K.dma("pool", [(wt.ap[:, :, 0:128], wsrc[:, :, O_AKV:O_AKV + 128]),
                       (wt.ap[:, :, 128:208], wsrc[:, :, O_IK:O_IK + 80])], [wbuf], [w_in], chn)
        for tb in range(NB):
            ps = next_ps(R)
            for k in range(KC):
                K.mm(ps[:, 0:208], tok_lhsT(k, tb), wt[:, k, 0:208], k == 0, k == KC - 1)
            ta = tmA[tb % 2]
            tw = tmW[tb % 2]
            K.act(tjunk[:, 0:128], ps[:, 0:128], AF.Square, accum=tstat[:, 0:1])
            K.act(tstat[:, 1:2], tstat[:, 0:1], AF.Sqrt, bias=R.epsc[:, 0:1], scale=1.0 / 128)
            K.op("dve", lambda e: e.reciprocal(out=tstat.ap[:, 2:3], in_=tstat.ap[:, 1:2]), [tstat], [tstat])
            K.stt("dve", ta[:, 0:128], ps[:, 0:128], tstat[:, 2:3], kvn_bc, ALU.mult, ALU.mult)
            K.act(tjunk[:, 0:64], ps[:, 128:192], AF.Square, accum=tstat[:, 3:4])
            K.act(tstat[:, 4:5], tstat[:, 3:4], AF.Sqrt, bias=R.epsc[:, 0:1], scale=1.0 / 64)
            K.op("dve", lambda e: e.reciprocal(out=tstat.ap[:, 5:6], in_=tstat.ap[:, 4:5]), [tstat], [tstat])
            K.stt("dve", ta[:, 128:192], ps[:, 128:192], tstat[:, 5:6], ikn_bc, ALU.mult, ALU.mult)
            K.ts("dve", tw, ps[:, 192:208], 1.0 / 32, None, ALU.mult)
            K.dma("sp", [(o_lat.ap[tb * 128:(tb + 1) * 128, :], ta.ap[:, 0:128])], [o_lat], [ta], "tmA%d" % (tb % 2))
            K.dma("sp", [(o_iw.ap[tb * 128:(tb + 1) * 128, :], tw.ap)], [o_iw], [tw], "tmW%d" % (tb % 2))
            pb = R.psb
            K.op("pe", lambda e: e.transpose(pb.ap[:, 0:128], ta.ap[:, 0:128], R.ident.ap), [pb], [ta, R.ident])
            K.op("pe", lambda e: e.transpose(pb.ap[0:64, 128:256], ta.ap[:, 128:192], R.ident.ap), [pb], [ta, R.ident])
            st, sch = next_stg(R)
            K.copy("act", st[:, 0:128], pb[:, 0:128])
            K.copy("act", st[0:64, 128:256], pb[0:64, 128:256])
            K.dma("sp", [(o_latT.ap[:, tb * 128:(tb + 1) * 128], st.ap[:, 0:128]),
                         (o_ikT.ap[:, tb * 128:(tb + 1) * 128], st.ap[0:64, 128:256])], [o_latT, o_ikT], [st], sch)
        for (c0, ncol, dst, dc0) in ((O_BV, 512, o_bv, 0), (O_BV + 512, 256, o_bv, 512), (O_CV, 128, o_cv, 0)):
            wbuf, chn = next_wb(R)
            wt = View(wbuf.ap.rearrange("p (k c) -> p k c", k=KC), wbuf.tu)
            K.dma("pool", [(wt.ap[:, :, 0:ncol], wsrc[:, :, c0:c0 + ncol])], [wbuf], [w_in], chn)
            for tb in range(NB):
                ps = next_ps(R)
                for k in range(KC):
                    K.mm(ps[:, 0:ncol], tok_lhsT(k, tb), wt[:, k, 0:ncol], k == 0, k == KC - 1)
                st, sch = next_stg(R)
                if tb % 2 == 0:
                    K.copy("dve", st[:, 0:ncol], ps[:, 0:ncol])
                else:
                    K.copy("act", st[:, 0:ncol], ps[:, 0:ncol])
                K.dma("sp", [(dst.ap[tb * 128:(tb + 1) * 128, dc0:dc0 + ncol], st.ap[:, 0:ncol])], [dst], [st], sch)
        K.finish()
    return nc


def emit_mod_body(K, R, wada_d, nvec):
    ncolc = nvec * KC
    psm = R.psf[6]
    wsrc = wada_d.ap.rearrange("(k p) c -> p k c", p=128)
    for piece in range(ncolc // 2):
        wbuf, ch = next_wb(R)
        wt = View(wbuf.ap.bitcast(F32).rearrange("p (k c) -> p k c", k=KC), wbuf.tu)
        K.dma("sp", [(wt.ap, wsrc[:, :, piece * 256:(piece + 1) * 256])], [wbuf], [wada_d], ch)
        for j in range(2):
            m = piece * 2 + j
            for k in range(KC):
                K.mm(psm[:, m:m + 1], wt[:, k, j * 128:(j + 1) * 128], R.cond[:, k:k + 1], k == 0, k == KC - 1)
    K.tt("dve", R.modT[:, 0:ncolc], psm[:, 0:ncolc], R.bada[:, 0:ncolc], ALU.add)


NIT = 26
B_GROUPS = ((128, 1), (512, 4), (2048, 16))
B_UNITS = [(g, o) for g in range(3) for o in range(B_GROUPS[g][0] // 128 + 1)]
NSTRIP = 21


def build_B(TOK, last, debug=False):
    nc = bass.Bass("TRN2", target_bir_lowering=False)
    es = contextlib.ExitStack()
    with es:
        K = Ctx(nc, es)
        TN = min(512, TOK)
        NT = TOK // TN
        NB = TOK // 128
        S = NCORES * TOK
        hin = dram_in(nc, "hT_in", [D, TOK])
        c_d = dram_in(nc, "c", [128, KC])
        wada = dram_in(nc, "w_ada", [D, 4 * D])
        bada = dram_in(nc, "b_ada", [128, 4 * KC])
        n3_d = dram_in(nc, "norm_ffn2", [128, KC])
        fn_d = dram_in(nc, "final_norm", [128, KC])
        f_in = dram_in(nc, "ffn_w_in", [D, 2 * FFN])
        f_out = dram_in(nc, "ffn_w_out", [FFN, D])
        wbr_a = dram_in(nc, "w_br_a", [768, D])
        wbr_b = dram_in(nc, "w_br_b", [256, D])
        wbr_c = dram_in(nc, "w_br_c", [512, D])
        wo_d = dram_in(nc, "w_out", [D, D])
        wuk_d = dram_in(nc, "w_uk", [128, 768])
        wuv_d = dram_in(nc, "w_uv", [128, 768])
        sink_d = dram_in(nc, "sinks_fm", [128, 4])
        ident_d = dram_in(nc, "ident", [128, 128])
        aqT_d = dram_in(nc, "aqT", [768, TOK], BF16)
        iqT_d = dram_in(nc, "iqT", [1024, TOK], BF16)
        iw_d = dram_in(nc, "iw", [TOK, 16], F32)
        bqT_d = dram_in(nc, "bqT", [768, TOK], BF16)
        cqT_d = dram_in(nc, "cqT", [512, TOK], BF16)
        sgT_d = dram_in(nc, "sgT", [3 * D, TOK], BF16)
        latT_d = dram_in(nc, "latT", [128, S], BF16)
        lat_d = dram_in(nc, "lat", [S, 128], BF16)
        ikT_d = dram_in(nc, "ikT", [64, S], BF16)
        bkw_d = dram_in(nc, "bk_win", [NB, 17, 768, 128], BF16)
        bvw_d = dram_in(nc, "bv_win", [NB, 17, 128, 768], BF16)
        ckw_d = dram_in(nc, "ck_win", [NB, 2, 128, 128], BF16)
        cvw_d = dram_in(nc, "cv_win", [NB, 2, 128, 128], BF16)
        cmask_d = dram_in(nc, "cmask", [128, 1024])
        strip_d = dram_in(nc, "stripA", [3, 128, NSTRIP * 512])
        mbB_d = dram_in(nc, "mbB", [3, len(B_UNITS), 128, 512])
        mbC_d = dram_in(nc, "mbC", [2, 4, 128, 512])
        pw2_d = dram_in(nc, "pw2", [128, NIT])
        hout = dram_out(nc, "outT" if last else "hT_out", [D, TOK])
        preT_d = View(nc.dram_tensor("preT", [12 * 128, TOK], BF16, kind="ExternalOutput" if debug else "Internal").ap())

        R = setup_persist(K, TOK)
        expsink = View(K.sb("expsink", [128, 4], F32)[:])
        sinkraw = View(K.sb("sinkraw", [128, 4], F32)[:])
        K.dma("sp", [(R.craw.ap, c_d.ap)], [R.craw], [c_d], "misc")
        K.dma("sp", [(R.bada.ap[:, 0:4 * KC], bada.ap)], [R.bada], [bada], "misc")
        load_vec(K, R.nrm[0], n3_d)
        load_vec(K, R.nrm[1], fn_d)
        K.dma("pool", [(R.ident.ap, ident_d.ap)], [R.ident], [ident_d], "misc")
        K.dma("sp", [(sinkraw.ap, sink_d.ap)], [sinkraw], [sink_d], "misc")
        K.misc_done()
        K.act(R.cond, R.craw, AF.Silu)
        K.act(expsink, sinkraw, AF.Exp)

        with scope(K):
            score = View(K.sb("score", [128, S], F32)[:])
            negsel = View(K.sb("negsel", [128, S], BF16)[:])
            latT = View(K.sb("latTs", [128, S], BF16)[:])
            latK = View(K.sb("latKs", [128, S // 128, 128], BF16)[:])
            regX = View(K.sb("regX", [128, NSTRIP * 512], BF16)[:])
            iqA = View(K.sb("iqA", [128, 8, 128], BF16)[:])
            iqB = View(K.sb("iqB", [128, 8, 128], BF16)[:])
            aqA = View(K.sb("aqA", [128, 6, 128], BF16)[:])
            aqB = View(K.sb("aqB", [128, 6, 128], BF16)[:])
            bq = View(K.sb("bq", [128, 6, 128], BF16)[:])
            cq = View(K.sb("cq", [128, 4, 128], BF16)[:])
            iw_s = View(K.sb("iw_s", [128, 16], F32)[:])
            qabs = View(K.sb("qabs", [128, 3, 512], BF16)[:])
            wukT = View(K.sb("wukT", [128, 6, 128], BF16)[:])
            wukn = View(K.sb("wukn", [128, 768], BF16)[:])
            wuv = View(K.sb("wuv", [128, 768], BF16)[:])
            BIGI4 = View(K.sb("BIGI4", [128, 512], BF16)[:])
            ones_bf = View(K.sb("ones_bf", [128, 128], BF16)[:])
            onesA = View(K.sb("onesA", [128, 128], BF16)[:])
            onesB = View(K.sb("onesB", [128, 128], BF16)[:])
            zer = View(K.sb("zer", [128, 128], BF16)[:])
            rbuf = [View(K.sb("rbuf%d" % i, [128, 512], F32)[:]) for i in range(2)]
            pbuf = [View(K.sb("pbuf%d" % i, [128, 512], BF16)[:]) for i in range(3)]
            stat = View(K.sb("stat", [128, 8], F32)[:])
            steps = View(K.sb("steps", [128, NIT], F32)[:])
            pw2 = View(K.sb("pw2s", [128, NIT], F32)[:])
            cmask = View(K.sb("cmasks", [128, 1024], F32)[:])
            rz = View(K.sb("rz", [128, 512], F32)[:])
            olat = View(K.sb("olat", [128, 512], BF16)[:])
            kBt = [View(K.sb("kBt%d" % i, [128, 4, 128], BF16)[:]) for i in range(2)]
            vBt = [View(K.sb("vBt%d" % i, [128, 4, 128], BF16)[:]) for i in range(2)]
            mbt = [View(K.sb("mbt%d" % i, [128, 512], BF16)[:]) for i in range(2)]
            kCt = [View(K.sb("kCt%d" % i, [128, 2, 128], BF16)[:]) for i in range(2)]
            vCt = [View(K.sb("vCt%d" % i, [128, 2, 128], BF16)[:]) for i in range(2)]
            prest = View(K.sb("prest", [128, 12, 128], BF16)[:])

            for v in (iqA, iqB, aqA, aqB, regX):
                K.memset("pool", v, 0.0)
            for v in kBt + vBt + kCt + vCt:
                K.memset("pool", v, 0.0)
            K.memset("pool", ones_bf, 1.0)
            K.memset("pool", zer, 0.0)
            K.memset("pool", onesA, 0.0)
            K.memset("pool", onesB, 0.0)
            K.memset("pool", onesA[:, 0:64], 1.0)
            K.memset("pool", onesB[:, 64:128], 1.0)
            for i in range(4):
                K.ts("dve", BIGI4[:, i * 128:(i + 1) * 128], R.ident, BIG, None, ALU.mult)
            K.dma("sp", [(pw2.ap, pw2_d.ap)], [pw2], [pw2_d], "misc")
            K.dma("sp", [(cmask.ap, cmask_d.ap)], [cmask], [cmask_d], "misc")
            K.dma("pool", [(wukn.ap, wuk_d.ap)], [wukn], [wuk_d], "misc")
            K.dma("pool", [(wuv.ap, wuv_d.ap)], [wuv], [wuv_d], "misc")
            K.misc_done()
            for c in range(6):
                K.op("pe", lambda e: e.transpose(R.psb.ap[:, c * 128:(c + 1) * 128], wukn.ap[:, c * 128:(c + 1) * 128], R.ident.ap),
                     [R.psb], [wukn, R.ident])
            K.copy("dve", View(wukT.ap.rearrange("p c r -> p (c r)"), wukT.tu), R.psb[:, 0:768])

            iqv = iqT_d.ap.rearrange("(c p) t -> p c t", p=128)
            aqv = aqT_d.ap.rearrange("(c p) t -> p c t", p=128)
            bqv = bqT_d.ap.rearrange("(c p) t -> p c t", p=128)
            cqv = cqT_d.ap.rearrange("(c p) t -> p c t", p=128)
            latv = lat_d.ap.rearrange("(b p) r -> p b r", p=128)
            prev = preT_d.ap.rearrange("(c p) t -> p c t", p=128)
            pi = 0
            for j in range(NB):
                sl = slice(j * 128, (j + 1) * 128)
                NKB = 8 * (j + 1)
                L = NKB * 128
                k0 = 8 * j * 128
                K.dma("sp", [(iqA.ap[0:64, :, :], iqv[0:64, :, sl]), (iqB.ap[64:128, :, :], iqv[64:128, :, sl])], [iqA, iqB], [iqT_d], "iq")
                K.dma("sp", [(aqA.ap[0:64, :, :], aqv[0:64, :, sl]), (aqB.ap[64:128, :, :], aqv[64:128, :, sl])], [aqA, aqB], [aqT_d], "aq")
                K.dma("sp", [(bq.ap, bqv[:, :, sl]), (cq.ap, cqv[:, :, sl])], [bq, cq], [bqT_d, cqT_d], "bcq")
                K.dma("sp", [(iw_s.ap, iw_d.ap[sl, :])], [iw_s], [iw_d], "iws")
                K.dma("sp", [(regX.ap[0:64, 0:L], ikT_d.ap[:, 0:L]), (regX.ap[64:128, 0:L], ikT_d.ap[:, 0:L])], [regX], [ikT_d], "regX")
                K.dma("sp", [(latT.ap[:, k0:k0 + 1024], latT_d.ap[:, k0:k0 + 1024]),
                             (latK.ap[:, 8 * j:8 * j + 8, :], latv[:, 8 * j:8 * j + 8, :])], [latT, latK], [latT_d, lat_d], "lat")
                for kc in range(L // 512):
                    ks = slice(kc * 512, (kc + 1) * 512)
                    for h in range(IDX_HEADS):
                        ps = next_ps(R, 3)
                        src = iqA if h % 2 == 0 else iqB
                        K.mm(ps, src[:, h // 2, :], regX[:, ks], True, True)
                        rb = rbuf[h % 2]
                        K.act(rb, ps, AF.Relu)
                        if h == 0:
                            K.ts("dve", score[:, ks], rb, iw_s[:, 0:1], None, ALU.mult)
                        else:
                            K.stt("dve", score[:, ks], rb, iw_s[:, h:h + 1], score[:, ks], ALU.mult, ALU.add)
                K.op("dve", lambda e: e.tensor_reduce(out=stat.ap[:, 0:1], in_=score.ap[:, 0:L], axis=AX.X, op=ALU.min), [stat], [score])
                K.tt("dve", score[:, L - 1024:L], score[:, L - 1024:L], cmask, ALU.add)
                K.op("dve", lambda e: e.tensor_reduce(out=stat.ap[:, 1:2], in_=score.ap[:, 0:L], axis=AX.X, op=ALU.max), [stat], [score])
                K.tt("dve", stat[:, 2:3], stat[:, 1:2], stat[:, 0:1], ALU.subtract)
                K.ts("dve", steps, pw2, stat[:, 2:3], None, ALU.mult)
                K.copy("dve", stat[:, 3:4], stat[:, 0:1])
                for it in range(NIT):
                    K.tt("dve", stat[:, 4:5], stat[:, 3:4], steps[:, it:it + 1], ALU.add)
                    K.ts("dve", negsel[:, 0:L], score[:, 0:L], stat[:, 4:5], 0.0, ALU.is_ge, ALU.add, accum=stat[:, 5:6])
                    K.ts("dve", stat[:, 6:7], stat[:, 5:6], TOPK - 0.5, steps[:, it:it + 1], ALU.is_ge, ALU.mult)
                    K.tt("dve", stat[:, 3:4], stat[:, 3:4], stat[:, 6:7], ALU.add)
                K.ts("dve", negsel[:, 0:L], score[:, 0:L], stat[:, 3:4], -1.0, ALU.is_ge, ALU.add)
                for hg in range(3):
                    ps = R.psf[5]
                    for hh in range(4):
                        h = hg * 4 + hh
                        src = aqA if h % 2 == 0 else aqB
                        K.mm(ps[:, hh * 128:(hh + 1) * 128], wukT[:, h // 2, :], src[:, h // 2, :], True, True)
                    K.ts("dve", qabs[:, hg, :], ps, 0.125, None, ALU.mult)
                for hg in range(3):
                    K.dma("pool", [(regX.ap, strip_d.ap[hg])], [regX], [strip_d], "regX")
                    Ops, Zps = R.psf[3], R.psf[4]
                    for kb in range(NKB):
                        kbs = slice(kb * 128, (kb + 1) * 128)
                        mi = min(8 * j + 7 - kb, NSTRIP - 1)
                        Lp = next_ps(R, 3)
                        K.mm(Lp, latT[:, kbs], qabs[:, hg, :], True, False)
                        K.mm(Lp, negsel[:, kbs], BIGI4, False, False)
                        K.mm(Lp, R.ident, regX[:, mi * 512:(mi + 1) * 512], False, True)
                        pb = pbuf[pi % 3]
                        pi += 1
                        K.act(pb, Lp, AF.Exp)
                        K.mm(Ops, latK[:, kb, :], pb, kb == 0, kb == NKB - 1)
                        K.mm(Zps, ones_bf, pb, kb == 0, kb == NKB - 1)
                    K.op("dve", lambda e: e.reciprocal(out=rz.ap, in_=Zps.ap), [rz], [Zps])
                    K.tt("dve", olat, Ops, rz, ALU.mult)
                    ps = R.psf[5]
                    for hh in range(4):
                        c = (hg * 4 + hh) // 2
                        K.mm(ps[:, hh * 128:(hh + 1) * 128], wuv[:, c * 128:(c + 1) * 128], olat[:, hh * 128:(hh + 1) * 128], True, True)
                    for hh in range(4):
                        c = (hg * 4 + hh) // 2
                        r0 = (hh % 2) * 64
                        eng = "act" if hh % 2 == 0 else "dve"
                        K.copy(eng, prest[r0:r0 + 64, c, :], ps[r0:r0 + 64, hh * 128:(hh + 1) * 128])
                jb = min(j, 2)
                accB = R.psf[6]
                for ui, (g, o) in enumerate(B_UNITS):
                    kt, vt, mt = kBt[ui % 2], vBt[ui % 2], mbt[ui % 2]
                    ksrc = bkw_d.ap[j, o].rearrange("(c p) k -> p c k", p=128)
                    vsrc = bvw_d.ap[j, o].rearrange("s (h d) -> s h d", d=64)
                    K.dma("sp", [(kt.ap[0:64, 0:4:2, :], ksrc[0:64, 2 * g:2 * g + 2, :]),
                                 (kt.ap[64:128, 1:4:2, :], ksrc[64:128, 2 * g:2 * g + 2, :])], [kt], [bkw_d], "kBt%d" % (ui % 2))
                    K.dma("sp", [(vt.ap[:, 0:4:2, 0:64], vsrc[:, 4 * g:4 * g + 4:2, :]),
                                 (vt.ap[:, 1:4:2, 64:128], vsrc[:, 4 * g + 1:4 * g + 4:2, :])], [vt], [bvw_d], "vBt%d" % (ui % 2))
                    K.dma("pool", [(mt.ap, mbB_d.ap[jb, ui])], [mt], [mbB_d], "mbt%d" % (ui % 2))
                    Lp = next_ps(R, 3)
                    K.mm(Lp, R.ident, mt, True, False)
                    for hh in range(4):
                        K.mm(Lp[:, hh * 128:(hh + 1) * 128], kt[:, hh, :], bq[:, 2 * g + hh // 2, :], False, hh == 3)
                    pb = pbuf[pi % 3]
                    pi += 1
                    K.act(pb, Lp, AF.Exp)
                    lastu = (ui == len(B_UNITS) - 1)
                    if ui == 0:
                        K.mm(accB, zer, BIGI4, True, False)
                    for hh in range(4):
                        pr = hh // 2
                        K.mm(accB[:, pr * 128:(pr + 1) * 128], vt[:, hh, :], pb[:, hh * 128:(hh + 1) * 128],
                             False, lastu and hh % 2 == 1)
                        K.mm(accB[:, 256 + pr * 128:256 + (pr + 1) * 128], onesA if hh % 2 == 0 else onesB, pb[:, hh * 128:(hh + 1) * 128],
                             False, lastu and hh % 2 == 1)
                K.op("dve", lambda e: e.reciprocal(out=rz.ap[:, 0:256], in_=accB.ap[:, 256:512]), [rz], [accB])
                K.tt("dve", View(prest.ap[:, 6:8, :].rearrange("p c q -> p (c q)"), prest.tu), accB[:, 0:256], rz[:, 0:256], ALU.mult)
                jc = min(j, 1)
                accO, accZ = R.psf[3], R.psf[4]
                nstep = 0
                for o in range(2):
                    for kv in range(2):
                        kt, vt, mt = kCt[nstep % 2], vCt[nstep % 2], mbt[nstep % 2]
                        K.dma("sp", [(kt.ap[0:64, 0, :], ckw_d.ap[j, o, kv * 64:(kv + 1) * 64, :]),
                                     (kt.ap[64:128, 1, :], ckw_d.ap[j, o, kv * 64:(kv + 1) * 64, :])], [kt], [ckw_d], "kCt%d" % (nstep % 2))
                        K.dma("sp", [(vt.ap[:, 0, 0:64], cvw_d.ap[j, o, :, kv * 64:(kv + 1) * 64]),
                                     (vt.ap[:, 1, 64:128], cvw_d.ap[j, o, :, kv * 64:(kv + 1) * 64])], [vt], [cvw_d], "vCt%d" % (nstep % 2))
                        K.dma("pool", [(mt.ap, mbC_d.ap[jc, kv * 2 + o])], [mt], [mbC_d], "mbt%d" % (nstep % 2))
                        Lp = next_ps(R, 3)
                        K.mm(Lp, R.ident, mt, True, False)
                        for g in range(4):
                            K.mm(Lp[:, g * 128:(g + 1) * 128], kt[:, g % 2, :], cq[:, kv * 2 + g // 2, :], False, g == 3)
                        pb = pbuf[pi % 3]
                        pi += 1
                        K.act(pb, Lp, AF.Exp)
                        if nstep == 0:
                            K.mm(accO, zer, BIGI4, True, False)
                            K.mm(accZ, zer, BIGI4, True, False)
                        for g in range(4):
                            ch = kv * 2 + g // 2
                            sp_ = (o == 1 and g % 2 == 1)
                            K.mm(accO[:, ch * 128:(ch + 1) * 128], vt[:, g % 2, :], pb[:, g * 128:(g + 1) * 128], False, sp_)
                            K.mm(accZ[:, ch * 128:(ch + 1) * 128], onesA if g % 2 == 0 else onesB, pb[:, g * 128:(g + 1) * 128], False, sp_)
                        nstep += 1
                for ch in range(4):
                    K.ts("dve", rz[:, ch * 128:(ch + 1) * 128], accZ[:, ch * 128:(ch + 1) * 128], expsink[:, ch:ch + 1], None, ALU.add)
                K.op("dve", lambda e: e.reciprocal(out=rz.ap, in_=rz.ap), [rz], [rz])
                K.tt("dve", View(prest.ap[:, 8:12, :].rearrange("p c q -> p (c q)"), prest.tu), accO, rz, ALU.mult)
                K.dma("sp", [(prev[:, :, sl], prest.ap)], [preT_d], [prest], "prest")

        with scope(K):
            alloc_ffn(K, R)
            gate = [View(K.sb("gate%d" % i, [128, 3, TN], BF16)[:]) for i in range(2)]
            tm1 = View(K.sb("tm1", [128, TN], F32)[:])
            tm2 = View(K.sb("tm2", [128, TN], F32)[:])
            hsrc = hin.ap.rearrange("(k p) t -> p k t", p=128)
            allh = [R.hT[c][t] for c in range(KC) for t in range(NT)]
            K.dma("sp", [(R.hT_t[:, c, :], hsrc[:, c, :]) for c in range(KC)], allh, [hin], "hload")
            allg = [R.gT[c][t] for c in range(12) for t in range(NT)]
            K.dma("sp", [(R.gT_t[:, 0:12, :], preT_d.ap.rearrange("(c p) t -> p c t", p=128))], allg, [preT_d], "preload")
            emit_mod_body(K, R, wada, 4)
            M = R.modT
            K.stt("dve", R.svec[0], M[:, 32:48], 1.0, R.nrm[0], ALU.add, ALU.mult)
            K.ts("dve", R.hgv[0], M[:, 48:64], 0.5, None, ALU.mult)
            sgv = sgT_d.ap.rearrange("(b k p) t -> p b k t", p=128, b=3)
            gi = 0
            for dp in range(KC // 2):
                wbuf, chn = next_wb(R)
                wt = View(wbuf.ap[:, 0:12 * 256].rearrange("p (f c) -> p f c", f=12), wbuf.tu)
                dsl = slice(dp * 256, (dp + 1) * 256)
                K.dma("pool", [(wt.ap[:, 0:6, :], wbr_a.ap[:, dsl].rearrange("(f p) c -> p f c", p=128)),
                               (wt.ap[:, 6:8, :], wbr_b.ap[:, dsl].rearrange("(f p) c -> p f c", p=128)),
                               (wt.ap[:, 8:12, :], wbr_c.ap[:, dsl].rearrange("(f p) c -> p f c", p=128))], [wbuf], [wbr_a, wbr_b, wbr_c], chn)
                for dd in range(2):
                    d = dp * 2 + dd
                    for t in range(NT):
                        gt = gate[gi % 2]
                        gi += 1
                        K.dma("sp", [(gt.ap, sgv[:, :, d, t * TN:(t + 1) * TN])], [gt], [sgT_d], "gate%d" % ((gi - 1) % 2))
                        pa, pbb, pc = next_ps(R), next_ps(R), next_ps(R)
                        for (pp, f0, nf) in ((pa, 0, 6), (pbb, 6, 2), (pc, 8, 4)):
                            for f in range(nf):
                                K.mm(pp[:, 0:TN], wt[:, f0 + f, dd * 128:(dd + 1) * 128], R.gT[f0 + f][t], f == 0, f == nf - 1)
                        K.tt("dve", tm1, pa[:, 0:TN], gt[:, 0, :], ALU.mult)
                        K.tt("dve", tm2, pbb[:, 0:TN], gt[:, 1, :], ALU.mult)
                        K.tt("pool", tm1, tm1, tm2, ALU.add)
                        K.tt("dve", tm2, pc[:, 0:TN], gt[:, 2, :], ALU.mult)
                        K.tt("pool", R.uT[d][t], tm1, tm2, ALU.add)
            wov = wo_d.ap.rearrange("(k p) c -> p k c", p=128)
            for dp in range(KC // 2):
                wbuf, chn = next_wb(R)
                wt = View(wbuf.ap[:, 0:KC * 256].rearrange("p (k c) -> p k c", k=KC), wbuf.tu)
                K.dma("pool", [(wt.ap, wov[:, :, dp * 256:(dp + 1) * 256])], [wbuf], [wo_d], chn)
                for dd in range(2):
                    d = dp * 2 + dd
                    for t in range(NT):
                        py = next_ps(R)
                        for k in range(KC):
                            K.mm(py[:, 0:TN], wt[:, k, dd * 128:(dd + 1) * 128], R.uT[k][t], k == 0, k == KC - 1)
                        K.stt("dve", R.hT[d][t], py[:, 0:TN], M[:, d:d + 1], R.hT[d][t], ALU.mult, ALU.add)
            emit_norm(K, R, R.svec[0], M[:, 16:32])
            emit_ffn(K, R, f_in, f_out, R.hgv[0])
            hdst = hout.ap.rearrange("(k p) t -> p k t", p=128)
            if not last:
                K.dma("sp", [(hdst[:, c, :], R.hT_t[:, c, :]) for c in range(KC)], [hout], allh, "hstore")
            else:
                pss = R.psf[6]
                for t in range(NT):
                    for c in range(KC):
                        sq = R.sq[c % 2]
                        K.act(sq, R.hT[c][t], AF.Square)
                        K.mm(pss[:, 0:TN], R.onesD, sq, c == 0, c == KC - 1)
                    K.act(R.rstd, pss[:, 0:TN], AF.Sqrt, bias=R.epsc[:, 0:1])
                    K.op("dve", lambda e: e.reciprocal(out=R.rstd.ap, in_=R.rstd.ap), [R.rstd], [R.rstd])
                    for c in range(KC):
                        K.stt("dve", R.hT[c][t], R.hT[c][t], R.nrm[1][:, c:c + 1], R.rstd, ALU.mult, ALU.mult)
                K.dma("sp", [(hdst[:, c, :], R.hT_t[:, c, :]) for c in range(KC)], [hout], allh, "hstore")
            K.finish()
    return nc


BF = ml_dtypes.bfloat16


def fmaj(v):
    return np.ascontiguousarray(np.asarray(v, dtype=np.float32).reshape(-1, 128).T)


def t5_bucket_np(d):
    d = np.asarray(d)
    dl = np.maximum(d, 16).astype(np.float32)
    large = 16 + (np.log(dl / np.float32(16)) / np.float32(np.log(2048 / 16)) * np.float32(16)).astype(np.int32)
    large = np.minimum(large, 31)
    return np.where(d < 16, d, large).astype(np.int64)


def build_tables(i, t5):
    t5 = np.asarray(t5, dtype=np.float32)
    s = np.arange(128)[:, None]
    q = np.arange(128)[None, :]
    cm = np.zeros((128, 8, 128), np.float32)
    for r in range(8):
        if r > i:
            cm[:, r, :] = -1e30
        elif r == i:
            cm[:, r, :] = np.where(np.arange(128)[None, :] <= np.arange(128)[:, None], 0.0, -1e30)
    cm = cm.reshape(128, 1024)
    st = np.zeros((3, 128, NSTRIP, 4, 128), np.float32)
    for m in range(NSTRIP):
        dd = 128 * (m + i - 7) + q - s
        bk = t5_bucket_np(np.maximum(dd, 0))
        for h in range(12):
            st[h // 4, :, m, h % 4, :] = np.where(dd >= 0, t5[bk, h], 0.0)
    st = st.reshape(3, 128, NSTRIP * 512)
    mb = np.full((3, len(B_UNITS), 128, 4, 128), -BIG, np.float32)
    for jc in range(3):
        for ui, (g, o) in enumerate(B_UNITS):
            win, dil = B_GROUPS[g]
            dd = 128 * o + q - s
            ok = (dd >= 0) & (dd <= win) & (dd % dil == 0)
            if jc < 2 and o > 8 * jc + i:
                ok = np.zeros_like(ok)
            bk = t5_bucket_np(np.maximum(dd, 0))
            for hh in range(4):
                mb[jc, ui, :, hh, :] = np.where(ok, t5[bk, 12 + g * 4 + hh], -BIG)
    mb = mb.reshape(3, len(B_UNITS), 128, 512)
    mc = np.full((2, 4, 128, 4, 128), -BIG, np.float32)
    for jc in range(2):
        for kv in range(2):
            for o in range(2):
                dd = 128 * o + q - s
                ok = (dd >= 0) & (dd < 128)
                if jc == 0 and o > i:
                    ok = np.zeros_like(ok)
                bk = t5_bucket_np(np.maximum(dd, 0))
                for g in range(4):
                    mc[jc, kv * 2 + o, :, g, :] = np.where(ok, t5[bk, 24 + kv * 4 + g], -BIG)
    mc = mc.reshape(2, 4, 128, 512)
    return cm, st, mb, mc


def to_natural_fm(outs, name, NB):
    F_ = outs[0][name].shape[0]
    full = np.zeros((F_, NCORES * NB * 128), outs[0][name].dtype)
    for i in range(NCORES):
        for j in range(NB):
            b = 8 * j + i
            full[:, b * 128:(b + 1) * 128] = outs[i][name][:, j * 128:(j + 1) * 128]
    return full


def to_natural_tm(outs, name, NB):
    F_ = outs[0][name].shape[1]
    full = np.zeros((NCORES * NB * 128, F_), outs[0][name].dtype)
    for i in range(NCORES):
        for j in range(NB):
            b = 8 * j + i
            full[b * 128:(b + 1) * 128] = outs[i][name][j * 128:(j + 1) * 128]
    return full


def windows_fm(full, i, NB, nwin):
    F_ = full.shape[0]
    w = np.zeros((NB, nwin, F_, 128), full.dtype)
    for j in range(NB):
        for o in range(nwin):
            b = 8 * j + i - o
            if b >= 0:
                w[j, o] = full[:, b * 128:(b + 1) * 128]
    return w


def windows_tm(full, i, NB, nwin):
    F_ = full.shape[1]
    w = np.zeros((NB, nwin, 128, F_), full.dtype)
    for j in range(NB):
        for o in range(nwin):
            b = 8 * j + i - o
            if b >= 0:
                w[j, o] = full[b * 128:(b + 1) * 128]
    return w


def shard_tokens_fm(x2d, NB):
    outs = []
    for i in range(NCORES):
        rows = np.concatenate([x2d[(8 * j + i) * 128:(8 * j + i + 1) * 128] for j in range(NB)], axis=0)
        outs.append(np.ascontiguousarray(rows.T))
    return outs


def unshard_tokens_fm(outs, NB):
    Dd = outs[0].shape[0]
    full = np.zeros((NCORES * NB * 128, Dd), outs[0].dtype)
    for i in range(NCORES):
        for j in range(NB):
            b = 8 * j + i
            full[b * 128:(b + 1) * 128] = outs[i][:, j * 128:(j + 1) * 128].T
    return full


_PROG = {}


def get_prog(kind, TOK, last=False):
    key = (kind, TOK, last)
    if key not in _PROG:
        _PROG[key] = build_A(TOK) if kind == "A" else build_B(TOK, last)
    return _PROG[key]


def run_layer(hs, W, l, NB, tables, last):
    TOK = NB * 128
    ident = np.eye(128, dtype=np.float32)
    cfm = fmaj(W["c"][0])
    wada = W["w_ada"][l]
    bada = W["b_ada"][l]
    common_a = {"c": cfm, "w_ada": np.ascontiguousarray(wada[:, :5 * D]), "b_ada": fmaj(bada[:5 * D]),
                "norm_ffn1": fmaj(W["norm_ffn1"][l]), "norm_mix": fmaj(W["norm_mix"][l]),
                "ffn_w_in": W["ffn1_w_in"][l], "ffn_w_out": W["ffn1_w_out"][l], "w_in": W["w_in"][l],
                "kv_norm": W["kv_norm"][l], "idx_k_norm": W["idx_k_norm"][l], "ident": ident}
    _t0 = time.time()
    ncA = get_prog("A", TOK)
    resA = run_bass_kernel_spmd(ncA, [dict(common_a, hT_in=hs[i]) for i in range(NCORES)], core_ids=list(range(NCORES))).results
    print("[kernel] layer %d A done %.1fs" % (l, time.time() - _t0), file=sys.stderr, flush=True)
    _t0 = time.time()
    latT = to_natural_fm(resA, "latT", NB)
    ikT = to_natural_fm(resA, "ikT", NB)
    bkT = to_natural_fm(resA, "bkT", NB)
    ckT = to_natural_fm(resA, "ckT", NB)
    lat = to_natural_tm(resA, "lat", NB)
    bv = to_natural_tm(resA, "bv", NB)
    cv = to_natural_tm(resA, "cv", NB)
    sinks = np.asarray(W["sinks"][l], dtype=np.float32)
    sinks_fm = np.ascontiguousarray(np.repeat(sinks.reshape(4, 2), 64, axis=1).T)
    pw2 = np.ascontiguousarray(np.broadcast_to((0.5 ** np.arange(1, NIT + 1)).astype(np.float32)[None, :], (128, NIT)))
    common_b = {"c": cfm, "w_ada": np.ascontiguousarray(wada[:, 5 * D:]), "b_ada": fmaj(bada[5 * D:]),
                "norm_ffn2": fmaj(W["norm_ffn2"][l]), "final_norm": fmaj(W["final_norm"]),
                "ffn_w_in": W["ffn2_w_in"][l], "ffn_w_out": W["ffn2_w_out"][l],
                "w_br_a": W["w_br_a"][l], "w_br_b": W["w_br_b"][l], "w_br_c": W["w_br_c"][l], "w_out": W["w_out"][l],
                "w_uk": W["w_uk"][l], "w_uv": W["w_uv"][l], "sinks_fm": sinks_fm, "ident": ident,
                "latT": latT, "lat": lat, "ikT": ikT, "pw2": pw2}
    in_maps = []
    for i in range(NCORES):
        cm, st, mb, mc = tables[i]
        m = dict(common_b)
        m.update({"hT_in": resA[i]["hT_out"], "aqT": resA[i]["aqT"], "iqT": resA[i]["iqT"], "iw": resA[i]["iw"],
                  "bqT": resA[i]["bqT"], "cqT": resA[i]["cqT"], "sgT": resA[i]["sgT"],
                  "bk_win": windows_fm(bkT, i, NB, 17), "bv_win": windows_tm(bv, i, NB, 17),
                  "ck_win": windows_fm(ckT, i, NB, 2), "cv_win": windows_tm(cv, i, NB, 2),
                  "cmask": cm, "stripA": st, "mbB": mb, "mbC": mc})
        in_maps.append(m)
    ncB = get_prog("B", TOK, last)
    print("[kernel] layer %d host prep %.1fs" % (l, time.time() - _t0), file=sys.stderr, flush=True)
    _t0 = time.time()
    resB = run_bass_kernel_spmd(ncB, in_maps, core_ids=list(range(NCORES))).results
    print("[kernel] layer %d B done %.1fs" % (l, time.time() - _t0), file=sys.stderr, flush=True)
    return [r["outT" if last else "hT_out"] for r in resB], resA


def kernel(**inputs):
    x = np.asarray(inputs["x"], dtype=np.float32)
    S = x.shape[1]
    NB = S // 128 // NCORES
    W = {k: np.asarray(v) for k, v in inputs.items()}
    tables = [build_tables(i, W["t5_table"]) for i in range(NCORES)]
    hs = shard_tokens_fm(x[0], NB)
    depth = W["w_in"].shape[0]
    for l in range(depth):
        hs, _ = run_layer(hs, W, l, NB, tables, last=(l == depth - 1))
    out = unshard_tokens_fm(hs, NB)
    return out[None].astype(np.float32)
```
